# Optimizing a Trainium2 kernel written in Bass

```python
import math
import jax, jax.numpy as jnp
from jax import lax
import numpy as np

D_MODEL = 1024
BATCH = 4
SEQ = 4096
DEPTH = 4

N_META = 16
BLOCK = 128
N_A_LAYERS = DEPTH // 2
N_B_LAYERS = DEPTH - N_A_LAYERS
CONV_WIDTH = 31
N_HEADS = 8
HEAD_DIM = D_MODEL // (2 * N_HEADS)
V_DIM = 2 * HEAD_DIM
ROT_DIM = HEAD_DIM // 4
ROPE_THETA = 500000.0
PEER_HEADS = 8
PEER_KEYS = 128
PEER_EXPERTS = PEER_KEYS * PEER_KEYS
PEER_HALF = 128
PEER_QDIM = 2 * PEER_HALF
PEER_TOPK = 16
PEER_CHUNK = 128
EPS = 1e-6

kernel_name = "yoco_conformer_diffattn_peer"


def rmsnorm(x, g):
    xf = x.astype(jnp.float32)
    y = xf * lax.rsqrt(jnp.mean(xf * xf, axis=-1, keepdims=True) + EPS)
    return (y * g.astype(jnp.float32)).astype(x.dtype)


def layernorm(x, g, b):
    xf = x.astype(jnp.float32)
    mu = jnp.mean(xf, axis=-1, keepdims=True)
    xc = xf - mu
    y = xc * lax.rsqrt(jnp.mean(xc * xc, axis=-1, keepdims=True) + EPS)
    return (y * g.astype(jnp.float32) + b.astype(jnp.float32)).astype(x.dtype)


def partial_rope(t, pos):
    inv = ROPE_THETA ** (-jnp.arange(0, ROT_DIM, 2, dtype=jnp.float32) / ROT_DIM)
    ang = pos.astype(jnp.float32)[:, None] * inv[None, :]
    cos = jnp.cos(ang)[None, :, None, None, :]
    sin = jnp.sin(ang)[None, :, None, None, :]
    tr = t[..., :ROT_DIM].astype(jnp.float32)
    t1, t2 = tr[..., :ROT_DIM // 2], tr[..., ROT_DIM // 2:]
    rot = jnp.concatenate([t1 * cos - t2 * sin, t1 * sin + t2 * cos], axis=-1).astype(t.dtype)
    return jnp.concatenate([rot, t[..., ROT_DIM:]], axis=-1)


def conformer_conv(h, w1, b1, dw_w, dw_b, ln_g, ln_b, w2, b2):
    u = h @ w1 + b1
    a, g = jnp.split(u, 2, axis=-1)
    u = a * jax.nn.sigmoid(g)
    u = lax.conv_general_dilated(
        u, dw_w[:, None, :], window_strides=(1,), padding=[(CONV_WIDTH - 1, 0)],
        dimension_numbers=('NWC', 'WIO', 'NWC'), feature_group_count=D_MODEL) + dw_b
    u = jax.nn.silu(layernorm(u, ln_g, ln_b))
    return u @ w2 + b2


def peer(h, wq, sk1, sk2, u_tab, v_tab):
    B, L, D = h.shape
    tok = h.reshape(-1, PEER_CHUNK, D)

    def chunk(xc):
        q = (xc @ wq).reshape(PEER_CHUNK, PEER_HEADS, 2, PEER_HALF)
        s1 = jnp.einsum('chd,kd->chk', q[:, :, 0], sk1)
        s2 = jnp.einsum('chd,kd->chk', q[:, :, 1], sk2)
        v1, i1 = lax.top_k(s1, PEER_TOPK)
        v2, i2 = lax.top_k(s2, PEER_TOPK)
        cand_s = (v1[..., :, None] + v2[..., None, :]).reshape(PEER_CHUNK, PEER_HEADS, PEER_TOPK * PEER_TOPK)
        cand_i = (i1[..., :, None] * PEER_KEYS + i2[..., None, :]).reshape(PEER_CHUNK, PEER_HEADS, PEER_TOPK * PEER_TOPK)
        top_s, top_pos = lax.top_k(cand_s, PEER_TOPK)
        expert = jnp.take_along_axis(cand_i, top_pos, axis=-1)
        gate = jax.nn.softmax(top_s.astype(jnp.float32), axis=-1).astype(xc.dtype)
        u = u_tab[expert]
        v = v_tab[expert]
        act = jax.nn.gelu(jnp.einsum('cd,chkd->chk', xc, u), approximate=False)
        return jnp.einsum('chk,chkd->cd', gate * act, v)

    return lax.map(chunk, tok).reshape(B, L, D)


def diff_attention(h, wq, k, v, lq1, lk1, lq2, lk2, subln_g, wo, lam_init, pos):
    B, L, D = h.shape
    q = (h @ wq).reshape(B, L, N_HEADS, 2, HEAD_DIM)
    q = partial_rope(q, pos) * (HEAD_DIM ** -0.5)
    lam = (jnp.exp(jnp.sum(lq1.astype(jnp.float32) * lk1.astype(jnp.float32)))
           - jnp.exp(jnp.sum(lq2.astype(jnp.float32) * lk2.astype(jnp.float32))) + lam_init)
    n_blk = L // BLOCK
    qb = q.reshape(B, n_blk, BLOCK, N_HEADS, 2, HEAD_DIM).transpose(1, 0, 2, 3, 4, 5)
    kpos = jnp.arange(L)

    def attend(args):
        qblk, bi = args
        s = jnp.einsum('bqhcd,bkhcd->bhcqk', qblk, k).astype(jnp.float32)
        qpos = bi * BLOCK + jnp.arange(BLOCK)
        mask = kpos[None, :] <= qpos[:, None]
        s = jnp.where(mask, s, -jnp.inf)
        p = jax.nn.softmax(s, axis=-1)
        a = p[:, :, 0] - lam * p[:, :, 1]
        return jnp.einsum('bhqk,bkhe->bqhe', a.astype(v.dtype), v)

    o = lax.map(attend, (qb, jnp.arange(n_blk)))
    o = o.transpose(1, 0, 2, 3, 4).reshape(B, L, N_HEADS, V_DIM)
    o = rmsnorm(o, subln_g) * (1.0 - lam_init)
    return o.reshape(B, L, N_HEADS * V_DIM) @ wo


def setup_inputs(seed: int = 0) -> dict:
    key = jax.random.key(seed)
    ks = jax.random.split(key, 32)
    D = D_MODEL
    nrm = lambda k, shape, scale: jax.random.normal(k, shape, jnp.float32) * scale
    gain = lambda k, shape: 1.0 + 0.01 * jax.random.normal(k, shape, jnp.float32)
    return {
        "x": nrm(ks[0], (BATCH, SEQ, D), 1.0),
        "meta_tokens": nrm(ks[1], (N_META, D), 1.0),
        "a_norm_g": gain(ks[2], (N_A_LAYERS, D)),
        "a_pw1_w": nrm(ks[3], (N_A_LAYERS, D, 2 * D), D ** -0.5),
        "a_pw1_b": nrm(ks[4], (N_A_LAYERS, 2 * D), 0.01),
        "a_dw_w": nrm(ks[5], (N_A_LAYERS, CONV_WIDTH, D), CONV_WIDTH ** -0.5),
        "a_dw_b": nrm(ks[6], (N_A_LAYERS, D), 0.01),
        "a_ln_g": gain(ks[7], (N_A_LAYERS, D)),
        "a_ln_b": nrm(ks[8], (N_A_LAYERS, D), 0.01),
        "a_pw2_w": nrm(ks[9], (N_A_LAYERS, D, D), D ** -0.5),
        "a_pw2_b": nrm(ks[10], (N_A_LAYERS, D), 0.01),
        "kv_norm_g": gain(ks[11], (D,)),
        "w_kv": nrm(ks[12], (D, 2 * N_HEADS * HEAD_DIM + N_HEADS * V_DIM), D ** -0.5),
        "b_norm_g": gain(ks[13], (N_B_LAYERS, D)),
        "b_wq": nrm(ks[14], (N_B_LAYERS, D, 2 * N_HEADS * HEAD_DIM), D ** -0.5),
        "b_lambda_q1": nrm(ks[15], (N_B_LAYERS, HEAD_DIM), 0.1),
        "b_lambda_k1": nrm(ks[16], (N_B_LAYERS, HEAD_DIM), 0.1),
        "b_lambda_q2": nrm(ks[17], (N_B_LAYERS, HEAD_DIM), 0.1),
        "b_lambda_k2": nrm(ks[18], (N_B_LAYERS, HEAD_DIM), 0.1),
        "b_subln_g": gain(ks[19], (N_B_LAYERS, V_DIM)),
        "b_wo": nrm(ks[20], (N_B_LAYERS, N_HEADS * V_DIM, D), D ** -0.5),
        "f_norm_g": gain(ks[21], (DEPTH, D)),
        "f_wq": nrm(ks[22], (DEPTH, D, PEER_HEADS * PEER_QDIM), D ** -0.5),
        "f_subkey1": nrm(ks[23], (DEPTH, PEER_KEYS, PEER_HALF), PEER_HALF ** -0.5),
        "f_subkey2": nrm(ks[24], (DEPTH, PEER_KEYS, PEER_HALF), PEER_HALF ** -0.5),
        "f_u": nrm(ks[25], (DEPTH, PEER_EXPERTS, D), D ** -0.5),
        "f_v": nrm(ks[26], (DEPTH, PEER_EXPERTS, D), D ** -0.5),
        "final_norm_g": gain(ks[27], (D,)),
    }


def reference(x, meta_tokens, a_norm_g, a_pw1_w, a_pw1_b, a_dw_w, a_dw_b, a_ln_g, a_ln_b,
              a_pw2_w, a_pw2_b, kv_norm_g, w_kv, b_norm_g, b_wq, b_lambda_q1, b_lambda_k1,
              b_lambda_q2, b_lambda_k2, b_subln_g, b_wo, f_norm_g, f_wq, f_subkey1, f_subkey2,
              f_u, f_v, final_norm_g):
    B, S, D = x.shape
    L = N_META + S
    Lp = -(-L // BLOCK) * BLOCK
    meta = jnp.broadcast_to(meta_tokens.astype(x.dtype)[None], (B, N_META, D))
    pad = jnp.zeros((B, Lp - L, D), x.dtype)
    h = jnp.concatenate([meta, x, pad], axis=1)
    pos = jnp.arange(Lp)
    k_shared = None
    v_shared = None
    for layer in range(DEPTH):
        if layer < N_A_LAYERS:
            i = layer
            h = h + conformer_conv(rmsnorm(h, a_norm_g[i]), a_pw1_w[i], a_pw1_b[i], a_dw_w[i],
                                   a_dw_b[i], a_ln_g[i], a_ln_b[i], a_pw2_w[i], a_pw2_b[i])
        else:
            j = layer - N_A_LAYERS
            lam_init = 0.8 - 0.6 * math.exp(-0.3 * layer)
            h = h + diff_attention(rmsnorm(h, b_norm_g[j]), b_wq[j], k_shared, v_shared,
                                   b_lambda_q1[j], b_lambda_k1[j], b_lambda_q2[j], b_lambda_k2[j],
                                   b_subln_g[j], b_wo[j], lam_init, pos)
        h = h + peer(rmsnorm(h, f_norm_g[layer]), f_wq[layer], f_subkey1[layer],
                     f_subkey2[layer], f_u[layer], f_v[layer])
        if layer == N_A_LAYERS - 1:
            kv = rmsnorm(h, kv_norm_g) @ w_kv
            k_flat, v_flat = jnp.split(kv, [2 * N_HEADS * HEAD_DIM], axis=-1)
            k_shared = partial_rope(k_flat.reshape(B, Lp, N_HEADS, 2, HEAD_DIM), pos)
            v_shared = v_flat.reshape(B, Lp, N_HEADS, V_DIM)
    out = rmsnorm(h, final_norm_g)
    return out[:, N_META:N_META + S]
```

```python
import math
import numpy as np
import concourse.bass as bass
import concourse.mybir as mybir
from concourse.bass_utils import run_bass_kernel_spmd
from contextlib import ExitStack

F32 = mybir.dt.float32
BF16 = mybir.dt.bfloat16
ALU = mybir.AluOpType
AF = mybir.ActivationFunctionType
AX = mybir.AxisListType

D = 1024
KC = 8
N_META = 16
SEQ = 4096
BATCH = 4
LP = 4224
NTB = 33
EPS = 1e-6
NEXP = 16384
NEG = -1.0e30

ENGS = ("sp", "act", "pool", "dve", "pe")


class Buf:
    __slots__ = ("name", "ap", "w", "r", "dsem", "dcnt")

    def __init__(self, name, ap):
        self.name = name
        self.ap = ap
        self.w = None
        self.r = {}
        self.dsem = None
        self.dcnt = 0

    def __getitem__(self, k):
        return self.ap[k]


class Sched:
    def __init__(self, nc, es):
        self.nc = nc
        self.es = es
        self.prog = {e: [] for e in ENGS}
        self.sem = {e: es.enter_context(nc.semaphore("prog_" + e)) for e in ENGS if e != "sp"}
        self.cnt = {e: 0 for e in ENGS}
        self.known = {e: {} for e in ENGS}
        self.ninst = 0
        self.nwait = 0
        self.dbufs = []
        self.dsem_pool = []
        self.nds = 0

    def _deps(self, eng, reads, writes):
        need = {}
        for b in reads:
            if b.w is not None:
                s, v = b.w
                if need.get(s, 0) < v:
                    need[s] = v
        own = self.sem.get(eng)
        raw_own = need.get(own, 0) if own is not None else 0
        for b in writes:
            if b.w is not None:
                s, v = b.w
                if need.get(s, 0) < v:
                    need[s] = v
            for s, v in b.r.items():
                if need.get(s, 0) < v:
                    need[s] = v
        out = []
        kn = self.known[eng]
        for s, v in need.items():
            if s is own:
                if raw_own == 0:
                    continue
                v = raw_own
            if kn.get(s, 0) >= v:
                continue
            kn[s] = v
            out.append((s, v))
        return out

    def _commit(self, tok, reads, writes):
        s, v = tok
        for b in reads:
            if b.r.get(s, 0) < v:
                b.r[s] = v
        for b in writes:
            b.w = tok
            b.r = {}

    def op(self, eng, meth, *args, reads=(), writes=(), **kwargs):
        fn = (meth, args, kwargs)
        waits = self._deps(eng, reads, writes)
        p = self.prog[eng]
        for s, v in waits:
            p.append(("w", s, v))
            self.nwait += 1
        self.cnt[eng] += 1
        tok = (self.sem[eng], self.cnt[eng])
        p.append(("i", fn, tok))
        self.ninst += 1
        self._commit(tok, reads, writes)
        return tok

    def dma(self, dst, out_ap, in_ap, reads=(), q="sp"):
        writes = (dst,)
        waits = self._deps(q, reads, writes)
        p = self.prog[q]
        for s, v in waits:
            p.append(("w", s, v))
            self.nwait += 1
        if dst.dsem is None:
            if self.dsem_pool:
                dst.dsem, dst.dcnt = self.dsem_pool.pop()
            else:
                dst.dsem = self.es.enter_context(self.nc.semaphore("d%d" % self.nds))
                self.nds += 1
                dst.dcnt = 0
            self.dbufs.append(dst)
        dst.dcnt += 16
        tok = (dst.dsem, dst.dcnt)
        p.append(("d", out_ap, in_ap, tok))
        self.ninst += 1
        self._commit(tok, reads, writes)
        return tok

    def _dtok(self, dst, q, reads, inc):
        writes = (dst,)
        waits = self._deps(q, reads, writes)
        p = self.prog[q]
        for s, v in waits:
            p.append(("w", s, v))
            self.nwait += 1
        if dst.dsem is None:
            if self.dsem_pool:
                dst.dsem, dst.dcnt = self.dsem_pool.pop()
            else:
                dst.dsem = self.es.enter_context(self.nc.semaphore("d%d" % self.nds))
                self.nds += 1
                dst.dcnt = 0
            self.dbufs.append(dst)
        dst.dcnt += inc
        tok = (dst.dsem, dst.dcnt)
        self.ninst += 1
        self._commit(tok, reads, writes)
        return tok

    def allgather(self, dst, out_ap, in_ap, reads=()):
        tok = self._dtok(dst, "pool", reads, 1)
        self.prog["pool"].append(("c", out_ap, in_ap, tok))
        return tok

    def dma_gather(self, dst, out_ap, in_ap, idx_ap, reads=()):
        tok = self._dtok(dst, "pool", reads, 16)
        self.prog["pool"].append(("g", out_ap, in_ap, idx_ap, tok))
        return tok

    def share_dsem(self, src, dst):
        dst.dsem = src.dsem
        dst.dcnt = src.dcnt

    def barrier(self):
        toks = {}
        for e in ENGS:
            if e != "sp" and self.cnt[e] > 0:
                toks[self.sem[e]] = self.cnt[e]
        for b in self.dbufs:
            if b.dcnt > 0:
                toks[b.dsem] = max(toks.get(b.dsem, 0), b.dcnt)
        for e in ENGS:
            own = self.sem.get(e)
            kn = self.known[e]
            for s, v in toks.items():
                if s is own:
                    continue
                if kn.get(s, 0) >= v:
                    continue
                kn[s] = v
                self.prog[e].append(("w", s, v))
                self.nwait += 1
        for b in self.dbufs:
            self.dsem_pool.append((b.dsem, b.dcnt))
            b.dsem = None
        self.dbufs = []

    def emit(self):
        nc = self.nc
        prog = self.prog

        def run(e, lst, fold):
            pend = []
            for it in lst:
                k = it[0]
                if k == "w":
                    pend.append(it)
                    continue
                att = None
                if pend and fold and k in ("i", "d"):
                    att = pend.pop()
                for w in pend:
                    e.wait_ge(w[1], w[2])
                pend = []
                if k == "i":
                    meth, args, kwargs = it[1]
                    ins = getattr(e, meth)(*args, **kwargs)
                    if att is not None:
                        ins._wait_ge(att[1], att[2])
                    ins.then_inc(it[2][0], 1)
                elif k == "d":
                    ins = e.dma_start(out=it[1], in_=it[2])
                    if att is not None:
                        ins._wait_ge(att[1], att[2])
                    ins.then_inc(it[3][0], 16)
                elif k == "c":
                    e.collective_compute("AllGather", ALU.bypass, replica_groups=[list(range(8))],
                                         ins=[it[2]], outs=[it[1]]).then_inc(it[3][0], 1)
                else:
                    e.indirect_dma_start(out=it[1], out_offset=None, in_=it[2],
                                         in_offset=bass.IndirectOffsetOnAxis(ap=it[3], axis=0)).then_inc(it[4][0], 16)
            for w in pend:
                e.wait_ge(w[1], w[2])

        with nc.Block() as block:
            @block.sync
            def _(e):
                run(e, prog["sp"], True)

            @block.scalar
            def _(e):
                run(e, prog["act"], True)

            @block.gpsimd
            def _(e):
                run(e, prog["pool"], False)

            @block.vector
            def _(e):
                run(e, prog["dve"], True)

            @block.tensor
            def _(e):
                run(e, prog["pe"], False)


class Arena:
    def __init__(self, tensor, nbytes):
        self.t = tensor
        self.nbytes = nbytes
        self.off = 0
        self.peak = 0

    def alloc(self, name, free_shape, dtype):
        esz = 2 if dtype == BF16 else 4
        n = 1
        for x in free_shape:
            n *= x
        nb = (n * esz + 63) // 64 * 64
        assert self.off + nb <= self.nbytes, (name, self.off, nb, self.nbytes)
        ap = self.t[:, self.off // 4:(self.off + nb) // 4]
        if dtype == BF16:
            ap = ap.bitcast(BF16)
        ap = ap[:, 0:n]
        if len(free_shape) == 2:
            ap = ap.rearrange("p (a b) -> p a b", b=free_shape[1])
        elif len(free_shape) == 3:
            ap = ap.rearrange("p (a b c) -> p a b c", b=free_shape[1], c=free_shape[2])
        elif len(free_shape) == 4:
            ap = ap.rearrange("p (a b c d) -> p a b c d", b=free_shape[1], c=free_shape[2], d=free_shape[3])
        self.off += nb
        self.peak = max(self.peak, self.off)
        return Buf(name, ap)

    def mark(self):
        return self.off

    def reset(self, m):
        self.off = m


def bcast(ap, axis, shape):
    return ap.unsqueeze(axis).to_broadcast(list(shape))


class Prog:
    def __init__(self, NT):
        self.NT = NT
        self.nc = bass.Bass("TRN2", target_bir_lowering=False)
        self.es = ExitStack()
        self.S = Sched(self.nc, self.es)
        nc = self.nc
        SB_BYTES = 206 * 1024
        self.sb_t = self.es.enter_context(nc.sbuf_tensor("arena", [128, SB_BYTES // 4], F32))
        self.A = Arena(self.sb_t, SB_BYTES)
        self.banks = [self.es.enter_context(nc.psum_tensor("bank%d" % i, [128, 512], F32)) for i in range(8)]
        self.inputs = {}
        self.dram_scratch = {}

    def din(self, name, shape):
        t = self.nc.dram_tensor(name, list(shape), F32, kind="ExternalInput")
        self.inputs[name] = tuple(shape)
        return t.ap()

    def dout(self, name, shape):
        return self.nc.dram_tensor(name, list(shape), F32, kind="ExternalOutput").ap()

    def dscratch(self, name, shape, dtype=F32):
        return self.nc.dram_tensor(name, list(shape), dtype, kind="Internal").ap()

    def debug(self, name, buf, shape, dtype=F32):
        if not getattr(self, "dbg", False):
            return
        t = self.nc.dram_tensor("dbg_" + name, list(shape), dtype, kind="ExternalOutput").ap()
        self.S.dma(Buf("dbg_" + name, t), t, buf.ap, reads=[buf])

    def pbank(self, name, i, dtype=F32, shape=None):
        ap = self.banks[i][:, :]
        if dtype == BF16:
            ap = ap.bitcast(BF16)
        if shape is not None:
            if len(shape) == 2:
                ap = ap[:, 0:shape[0] * shape[1]].rearrange("p (a b) -> p a b", b=shape[1])
            else:
                ap = ap[:, 0:shape[0]]
        return Buf(name, ap)

    def pbank2(self, name, i, shape):
        return [self.pbank(name + "_%d" % k, i + k, F32, shape) for k in range(2)]


def load_cast_weight(P, w_dram, n_out, wdst, stg, q="sp"):
    S = P.S
    wv = w_dram.rearrange("(kc p) n -> kc p n", p=128)
    for kc in range(KC):
        sb = stg[kc % len(stg)]
        S.dma(sb, sb.ap[:, 0:n_out], wv[kc])
        eng = "pool" if kc % 2 == 0 else "act"
        if eng == "pool":
            S.op("pool", "tensor_copy", wdst.ap[:, kc, :], sb.ap[:, 0:n_out], reads=[sb], writes=[wdst])
        else:
            S.op("act", "copy", wdst.ap[:, kc, :], sb.ap[:, 0:n_out], reads=[sb], writes=[wdst])


def rmsnorm_T(P, hT, gB, xn, junk, small, pT, identb, xnT_dst, evac="act"):
    S = P.S
    ss, r1, r2 = small
    S.op("dve", "scalar_tensor_tensor", out=junk.ap, in0=hT.ap, scalar=1.0, in1=hT.ap,
                                                  op0=ALU.mult, op1=ALU.mult, accum_out=ss.ap,
         reads=[hT], writes=[junk, ss])
    S.op("dve", "tensor_scalar", r1.ap, ss.ap, 1.0 / D, EPS, ALU.mult, ALU.add,
         reads=[ss], writes=[r1])
    S.op("act", "activation", out=r2.ap, in_=r1.ap, func=AF.Sqrt, reads=[r1], writes=[r2])
    S.op("dve", "reciprocal", r1.ap, r2.ap, reads=[r2], writes=[r1])
    S.op("dve", "scalar_tensor_tensor", out=xn.ap, in0=hT.ap, scalar=r1.ap, in1=gB.ap,
                                                  op0=ALU.mult, op1=ALU.mult,
         reads=[hT, r1, gB], writes=[xn])
    for kc in range(KC):
        S.op("pe", "transpose", pT.ap[:, kc, :], xn.ap[:, kc * 128:(kc + 1) * 128], identb.ap,
             reads=[xn, identb], writes=[pT])
    dstbuf, dstap = xnT_dst
    if evac == "act":
        S.op("act", "copy", dstap, pT.ap, reads=[pT], writes=[dstbuf])
    else:
        S.op("dve", "tensor_copy", dstap, pT.ap, reads=[pT], writes=[dstbuf])


def build_prog(NT, layers, NG=32, with_final=False, ST=4, dbg=False, kv=False, NKV=NTB, kv_after=None):
    P = Prog(NT)
    P.dbg = dbg
    S, A, nc = P.S, P.A, P.nc
    hin = P.din("hin", [NT * 128, D])
    hout = P.dout("hout", [NT * 128, D])
    ident_d = P.din("ident", [128, 128])

    h = [A.alloc("h%d" % s, [D], F32) for s in range(NT)]
    identf = A.alloc("identf", [128], F32)
    identb = A.alloc("identb", [128], BF16)
    onesm = A.alloc("onesm", [128], F32)
    S.dma(identf, identf.ap, ident_d)
    S.op("dve", "tensor_copy", identb.ap, identf.ap, reads=[identf], writes=[identb])
    S.op("pool", "memset", onesm.ap, 1.0 / D, writes=[onesm])
    hv = hin.rearrange("(s p) d -> s p d", p=128)
    for s in range(NT):
        S.dma(h[s], h[s].ap, hv[s])
    base_mark = A.mark()

    kvs = None
    if kv and kv_after is None:
        kvs = kv_phase(P, identb, NKV)
        S.barrier()
        A.reset(base_mark)

    for li, L in enumerate(layers):
        if kv and kv_after is not None and li == kv_after:
            hx_in_t = nc.dram_tensor("hx_in", [NT * 128, D], F32)
            hx_out_t = nc.dram_tensor("hx_out", [8 * NT * 128, D], F32)
            hx_in = Buf("hx_in", hx_in_t.ap())
            hx_out = Buf("hx_out", hx_out_t.ap())
            hxv = hx_in_t.ap().rearrange("(s p) d -> s p d", p=128)
            for s in range(NT):
                S.dma(hx_in, hxv[s], h[s].ap, reads=[h[s]])
            S.allgather(hx_out, hx_out_t.ap().opt(), hx_in_t.ap().opt(), reads=[hx_in])
            kvs = kv_phase(P, identb, NKV, gathered=hx_out)
            S.barrier()
            A.reset(base_mark)
        if L["kind"] == "conv":
            conv_layer(P, li, h, identb, onesm, ST)
            S.barrier()
            A.reset(base_mark)
        elif L["kind"] == "attn":
            attn_layer(P, li, L["lam_init"], h, identb, kvs, NKV)
            S.barrier()
            A.reset(base_mark)
        if L.get("peer", True):
            peer_layer(P, li, h, identb, identf, NG, ST)
            S.barrier()
            A.reset(base_mark)

    ov = hout.rearrange("(s p) d -> s p d", p=128)
    outb = Buf("hout", hout)
    if with_final:
        g_d = P.din("fin_g", [128, D])
        gB = A.alloc("gB", [D], F32)
        S.dma(gB, gB.ap, g_d)
        junk = A.alloc("junk", [D], BF16)
        res = [A.alloc("res%d" % i, [D], F32) for i in range(2)]
        small = [[A.alloc("sm%d_%d" % (i, j), [1], F32) for j in range(3)] for i in range(2)]
        for s in range(NT):
            ss, r1, r2 = small[s % 2]
            rb = res[s % 2]
            S.op("dve", "scalar_tensor_tensor", out=junk.ap, in0=h[s].ap, scalar=1.0, in1=h[s].ap,
                 op0=ALU.mult, op1=ALU.mult, accum_out=ss.ap, reads=[h[s]], writes=[junk, ss])
            S.op("dve", "tensor_scalar", r1.ap, ss.ap, 1.0 / D, EPS, ALU.mult, ALU.add, reads=[ss], writes=[r1])
            S.op("act", "activation", out=r2.ap, in_=r1.ap, func=AF.Sqrt, reads=[r1], writes=[r2])
            S.op("dve", "reciprocal", r1.ap, r2.ap, reads=[r2], writes=[r1])
            S.op("dve", "scalar_tensor_tensor", out=rb.ap, in0=h[s].ap, scalar=r1.ap, in1=gB.ap,
                 op0=ALU.mult, op1=ALU.mult, reads=[h[s], r1, gB], writes=[rb])
            S.dma(outb, ov[s], rb.ap, reads=[rb])
    else:
        for s in range(NT):
            S.dma(outb, ov[s], h[s].ap, reads=[h[s]])
    S.barrier()
    S.emit()
    return P


def rope_ops(P, x_sb, cs, tmp):
    S = P.S
    x4 = x_sb.ap.rearrange("p (m d) -> p m d", d=64)
    t1 = x4[:, :, 0:8]
    t2 = x4[:, :, 8:16]
    cosB = bcast(cs.ap[:, 0:8], 1, [128, 16, 8])
    sinB = bcast(cs.ap[:, 8:16], 1, [128, 16, 8])
    a, b, c, d = tmp
    S.op("dve", "tensor_tensor", a.ap, t1, cosB, ALU.mult, reads=[x_sb, cs], writes=[a])
    S.op("pool", "tensor_tensor", b.ap, t2, sinB, ALU.mult, reads=[x_sb, cs], writes=[b])
    S.op("dve", "tensor_tensor", c.ap, t1, sinB, ALU.mult, reads=[x_sb, cs], writes=[c])
    S.op("pool", "tensor_tensor", d.ap, t2, cosB, ALU.mult, reads=[x_sb, cs], writes=[d])
    S.op("dve", "tensor_tensor", t1, a.ap, b.ap, ALU.subtract, reads=[a, b], writes=[x_sb])
    S.op("dve", "tensor_tensor", t2, c.ap, d.ap, ALU.add, reads=[c, d], writes=[x_sb])


def kv_phase(P, identb, NKV, gathered=None):
    S, A = P.S, P.A
    if gathered is None:
        hf_d = P.din("hfull", [NKV * 128, D])
    else:
        idx_d = P.nc.dram_tensor("kv_idx", [128, NKV], mybir.dt.int32, kind="ExternalInput").ap()
        P.inputs["kv_idx"] = (128, NKV)
    g_d = P.din("kv_ng", [128, D])
    w_d = P.din("kv_w", [D, 2 * D])
    cs_d = P.din("kv_cs", [128, NKV * 16])
    KT = P.dscratch("KT", [16, 64, NKV * 128], BF16)
    VS = P.dscratch("VS", [NKV, 128, D], BF16)
    KTb = Buf("KT", KT)
    VSb = Buf("VS", VS)

    wkv = A.alloc("wkv", [KC, 2 * D], BF16)
    stg = [A.alloc("wstg%d" % i, [2 * D], F32) for i in range(2)]
    gB = A.alloc("gB", [D], F32)
    csb = A.alloc("kvcs", [NKV, 16], F32)
    hk = [A.alloc("hk%d" % i, [D], F32) for i in range(2)]
    xn = [A.alloc("xn%d" % i, [D], BF16) for i in range(2)]
    junk = A.alloc("junk", [D], BF16)
    small = [[A.alloc("sm%d_%d" % (i, j), [1], F32) for j in range(3)] for i in range(2)]
    xnT = [A.alloc("xnT%d" % i, [KC, 128], BF16) for i in range(2)]
    k_sb = [A.alloc("k_sb%d" % i, [D], F32) for i in range(2)]
    k_bf = [A.alloc("k_bf%d" % i, [D], BF16) for i in range(2)]
    v_bf = [A.alloc("v_bf%d" % i, [D], BF16) for i in range(2)]
    kT_sb = [A.alloc("kT_sb%d" % i, [16, 128], BF16) for i in range(2)]
    rtmp = [A.alloc("rtmp%d" % i, [16, 8], F32) for i in range(4)]
    cst = [A.alloc("cst%d" % i, [16], F32) for i in range(2)]

    S.dma(gB, gB.ap, g_d)
    S.dma(csb, csb.ap, cs_d.rearrange("p (t c) -> p t c", c=16))
    load_cast_weight(P, w_d, 2 * D, wkv, stg)

    pT = P.pbank("pT", 0, BF16, [KC, 128])
    ps_k = [P.pbank("ps_k%d" % i, 1 + i) for i in range(2)]
    ps_v = [P.pbank("ps_v%d" % i, 3 + i) for i in range(2)]
    ps_t = [P.pbank("ps_t%d" % i, 5 + i, BF16, [8, 128]) for i in range(2)]

    if gathered is None:
        hv = hf_d.rearrange("(s p) d -> s p d", p=128)
    else:
        idxb = A.alloc("kvidx", [NKV], F32)
        idx_ap = idxb.ap.bitcast(mybir.dt.int32)
        S.dma(idxb, idx_ap, idx_d)
    for t in range(NKV):
        i = t % 2
        if gathered is None:
            S.dma(hk[i], hk[i].ap, hv[t])
        else:
            S.dma_gather(hk[i], hk[i].ap, gathered.ap, idx_ap[:, t:t + 1], reads=[gathered, idxb])
        rmsnorm_T(P, hk[i], gB, xn[i], junk, small[i], pT, identb, (xnT[i], xnT[i].ap))
        for n in range(2):
            for kc in range(KC):
                S.op("pe", "matmul", ps_k[n].ap, xnT[i].ap[:, kc, :], wkv.ap[:, kc, n * 512:(n + 1) * 512],
                     start=(kc == 0), stop=(kc == KC - 1), reads=[xnT[i], wkv], writes=[ps_k[n]])
            S.op("act", "copy", k_sb[i].ap[:, n * 512:(n + 1) * 512], ps_k[n].ap, reads=[ps_k[n]], writes=[k_sb[i]])
        for n in range(2):
            for kc in range(KC):
                S.op("pe", "matmul", ps_v[n].ap, xnT[i].ap[:, kc, :], wkv.ap[:, kc, D + n * 512:D + (n + 1) * 512],
                     start=(kc == 0), stop=(kc == KC - 1), reads=[xnT[i], wkv], writes=[ps_v[n]])
            S.op("act", "copy", v_bf[i].ap[:, n * 512:(n + 1) * 512], ps_v[n].ap, reads=[ps_v[n]], writes=[v_bf[i]])
        S.dma(VSb, VS[t], v_bf[i].ap, reads=[v_bf[i]])
        S.op("pool", "tensor_copy", cst[i].ap, csb.ap[:, t, :], reads=[csb], writes=[cst[i]])
        rope_ops(P, k_sb[i], cst[i], rtmp)
        S.op("dve", "tensor_copy", k_bf[i].ap, k_sb[i].ap, reads=[k_sb[i]], writes=[k_bf[i]])
        for half in range(2):
            for m in range(8):
                hm = half * 8 + m
                S.op("pe", "transpose", ps_t[half].ap[0:64, m, :], k_bf[i].ap[:, hm * 64:(hm + 1) * 64], identb.ap,
                     reads=[k_bf[i], identb], writes=[ps_t[half]])
            S.op("act", "copy", kT_sb[i].ap[0:64, half * 8:(half + 1) * 8, :], ps_t[half].ap[0:64, :, :],
                 reads=[ps_t[half]], writes=[kT_sb[i]])
        S.dma(KTb, KT[:, :, t * 128:(t + 1) * 128].rearrange("m d t -> d m t"), kT_sb[i].ap[0:64, :, :], reads=[kT_sb[i]])
    return {"KT": KT, "KTb": KTb, "VS": VS, "VSb": VSb}


def attn_layer(P, li, lam_init, h, identb, kvs, NKV):
    S, A, NT = P.S, P.A, P.NT
    pre = "b%d_" % li
    g_d = P.din(pre + "ng", [128, D])
    wq_d = P.din(pre + "wq", [D, D])
    wo_d = P.din(pre + "wo", [D, D])
    lam_d = P.din(pre + "lam", [128, 256])
    sg_d = P.din(pre + "subg", [128, 128])
    qcs_d = P.din(pre + "qcs", [128, NT * 16])
    ekj_d = P.din(pre + "ekj", [NKV, NKV * 128])
    bpen_d = P.din(pre + "bpen", [NKV, NT * 128])
    tri_d = P.din(pre + "tri", [128, 256])
    QT = P.dscratch(pre + "QT", [16, 64, NT * 128], BF16)
    QTb = Buf("QT", QT)
    KT, KTb, VS, VSb = kvs["KT"], kvs["KTb"], kvs["VS"], kvs["VSb"]

    o_all = [A.alloc("o_all%d" % s, [D], BF16) for s in range(NT)]
    lamt = A.alloc("lamt", [256], F32)
    lprod = A.alloc("lprod", [2, 64], F32)
    lsum = A.alloc("lsum", [2], F32)
    lexp = A.alloc("lexp", [2], F32)
    nlam = A.alloc("nlam", [1], F32)
    subg = A.alloc("subg", [128], F32)
    ekjb = A.alloc("ekjb", [NKV * 128], BF16)
    bpenb = A.alloc("bpenb", [NT * 128], BF16)
    trib = A.alloc("trib", [256], BF16)
    S.dma(lamt, lamt.ap, lam_d)
    S.dma(subg, subg.ap, sg_d)
    l4 = lamt.ap.rearrange("p (a b d) -> p a b d", b=2, d=64)
    S.op("dve", "tensor_tensor", lprod.ap, l4[:, :, 0, :], l4[:, :, 1, :], ALU.mult, reads=[lamt], writes=[lprod])
    S.op("dve", "tensor_reduce", out=lsum.ap, in_=lprod.ap, axis=AX.X, op=ALU.add, reads=[lprod], writes=[lsum])
    S.op("act", "activation", out=lexp.ap, in_=lsum.ap, func=AF.Exp, reads=[lsum], writes=[lexp])
    S.op("dve", "tensor_tensor", nlam.ap, lexp.ap[:, 1:2], lexp.ap[:, 0:1], ALU.subtract, reads=[lexp], writes=[nlam])
    S.op("dve", "tensor_scalar", nlam.ap, nlam.ap, -float(lam_init), None, ALU.add, reads=[nlam], writes=[nlam])
    S.op("dve", "tensor_scalar", subg.ap, subg.ap, 1.0 - float(lam_init), None, ALU.mult, reads=[subg], writes=[subg])
    amark = A.mark()

    wq = A.alloc("wq", [KC, D], BF16)
    stg = [A.alloc("wstg%d" % i, [D], F32) for i in range(2)]
    gB = A.alloc("gB", [D], F32)
    qcs = A.alloc("qcs", [NT, 16], F32)
    xn = [A.alloc("xn%d" % i, [D], BF16) for i in range(2)]
    junk = A.alloc("junk", [D], BF16)
    small = [[A.alloc("sm%d_%d" % (i, j), [1], F32) for j in range(3)] for i in range(2)]
    xnT = [A.alloc("xnT%d" % i, [KC, 128], BF16) for i in range(2)]
    q_sb = [A.alloc("q_sb%d" % i, [D], F32) for i in range(2)]
    q_bf = [A.alloc("q_bf%d" % i, [D], BF16) for i in range(2)]
    qT_sb = [A.alloc("qT_sb%d" % i, [16, 128], BF16) for i in range(2)]
    rtmp = [A.alloc("rtmp%d" % i, [16, 8], F32) for i in range(4)]
    cst = [A.alloc("cst%d" % i, [16], F32) for i in range(2)]
    S.dma(gB, gB.ap, g_d)
    S.dma(qcs, qcs.ap, qcs_d.rearrange("p (t c) -> p t c", c=16))
    ekjf = A.alloc("ekjf", [NKV * 128], F32)
    bpenf = A.alloc("bpenf", [NT * 128], F32)
    trif = A.alloc("trif", [256], F32)
    S.dma(ekjf, ekjf.ap[0:NKV, :], ekj_d)
    S.dma(bpenf, bpenf.ap[0:NKV, :], bpen_d)
    S.dma(trif, trif.ap, tri_d)
    S.op("pool", "tensor_copy", ekjb.ap[0:NKV, :], ekjf.ap[0:NKV, :], reads=[ekjf], writes=[ekjb])
    S.op("pool", "tensor_copy", bpenb.ap[0:NKV, :], bpenf.ap[0:NKV, :], reads=[bpenf], writes=[bpenb])
    S.op("pool", "tensor_copy", trib.ap, trif.ap, reads=[trif], writes=[trib])
    load_cast_weight(P, wq_d, D, wq, stg)
    pT = P.pbank("pT", 0, BF16, [KC, 128])
    ps_q = [P.pbank("ps_q%d" % i, 1 + i) for i in range(2)]
    ps_t = [P.pbank("ps_t%d" % i, 3 + i, BF16, [8, 128]) for i in range(2)]
    for s in range(NT):
        i = s % 2
        rmsnorm_T(P, h[s], gB, xn[i], junk, small[i], pT, identb, (xnT[i], xnT[i].ap))
        for n in range(2):
            for kc in range(KC):
                S.op("pe", "matmul", ps_q[n].ap, xnT[i].ap[:, kc, :], wq.ap[:, kc, n * 512:(n + 1) * 512],
                     start=(kc == 0), stop=(kc == KC - 1), reads=[xnT[i], wq], writes=[ps_q[n]])
            S.op("act", "mul", q_sb[i].ap[:, n * 512:(n + 1) * 512], ps_q[n].ap, 0.125,
                 reads=[ps_q[n]], writes=[q_sb[i]])
        S.op("pool", "tensor_copy", cst[i].ap, qcs.ap[:, s, :], reads=[qcs], writes=[cst[i]])
        rope_ops(P, q_sb[i], cst[i], rtmp)
        S.op("dve", "tensor_copy", q_bf[i].ap, q_sb[i].ap, reads=[q_sb[i]], writes=[q_bf[i]])
        for half in range(2):
            for m in range(8):
                hm = half * 8 + m
                S.op("pe", "transpose", ps_t[half].ap[0:64, m, :], q_bf[i].ap[:, hm * 64:(hm + 1) * 64], identb.ap,
                     reads=[q_bf[i], identb], writes=[ps_t[half]])
            S.op("act", "copy", qT_sb[i].ap[0:64, half * 8:(half + 1) * 8, :], ps_t[half].ap[0:64, :, :],
                 reads=[ps_t[half]], writes=[qT_sb[i]])
        S.dma(QTb, QT[:, :, s * 128:(s + 1) * 128].rearrange("m d t -> d m t"), qT_sb[i].ap[0:64, :, :], reads=[qT_sb[i]])
    S.barrier()
    A.reset(amark)

    NK = NKV * 128
    KTh = [A.alloc("KTh%d" % i, [2, NK], BF16) for i in range(2)]
    QTh = [A.alloc("QTh%d" % i, [2, NT * 128], BF16) for i in range(2)]
    Va = [A.alloc("Va%d" % i, [NKV, 132], BF16) for i in range(2)]
    Psb = [A.alloc("Psb%d" % i, [2, 512], BF16) for i in range(2)]
    rec = A.alloc("rec", [2], F32)
    nl = A.alloc("nl", [1], F32)
    o1 = A.alloc("o1", [128], F32)
    o2 = A.alloc("o2", [128], F32)
    ojunk = A.alloc("ojunk", [128], F32)
    oss = A.alloc("oss", [1], F32)
    or1 = A.alloc("or1", [1], F32)
    or2 = A.alloc("or2", [1], F32)
    for i in range(2):
        S.op("pool", "memset", Va[i].ap[:, :, 128:132], 1.0, writes=[Va[i]])
    psO = [P.pbank("psO%d" % k, k, F32, [2, 129]) for k in range(4)]
    psS = [[P.pbank("psS%d_%d" % (i, c), 4 + 2 * i + c) for c in range(2)] for i in range(2)]
    n_sg = (NT + 3) // 4
    itS = 0
    for hd in range(8):
        hb = hd % 2
        for c in range(2):
            S.dma(KTh[hb], KTh[hb].ap[0:64, c, :], KT[2 * hd + c], reads=[KTb])
            S.dma(QTh[hb], QTh[hb].ap[0:64, c, :], QT[2 * hd + c], reads=[QTb])
        S.dma(Va[hb], Va[hb].ap[:, :, 0:128], VS[:, :, hd * 128:(hd + 1) * 128].rearrange("t p e -> p t e"), reads=[VSb])
        for sg in range(n_sg):
            s0 = sg * 4
            ns = min(4, NT - s0)
            Q = ns * 128
            for kj in range(NKV):
                ib = itS % 2
                itS += 1
                diag = [(k, 0) for k in range(ns) if kj == s0 + k] + [(k, 1) for k in range(ns) if kj == (NKV - NT) + s0 + k]
                for c in range(2):
                    S.op("pe", "matmul", psS[ib][c].ap[:, 0:Q], KTh[hb].ap[0:64, c, kj * 128:(kj + 1) * 128],
                         QTh[hb].ap[0:64, c, s0 * 128:s0 * 128 + Q], start=True, stop=False,
                         reads=[KTh[hb], QTh[hb]], writes=[psS[ib][c]])
                    S.op("pe", "matmul", psS[ib][c].ap[:, 0:Q], ekjb.ap[0:NKV, kj * 128:(kj + 1) * 128],
                         bpenb.ap[0:NKV, s0 * 128:s0 * 128 + Q], start=False, stop=(len(diag) == 0),
                         reads=[ekjb, bpenb], writes=[psS[ib][c]])
                    for di, (k, typ) in enumerate(diag):
                        S.op("pe", "matmul", psS[ib][c].ap[:, k * 128:(k + 1) * 128], identb.ap, trib.ap[:, typ * 128:(typ + 1) * 128],
                             start=False, stop=(di == len(diag) - 1), skip_group_check=True,
                             reads=[identb, trib], writes=[psS[ib][c]])
                    S.op("act", "activation", out=Psb[ib].ap[:, c, 0:Q], in_=psS[ib][c].ap[:, 0:Q], func=AF.Exp,
                         reads=[psS[ib][c]], writes=[Psb[ib]])
                for k in range(ns):
                    for c in range(2):
                        S.op("pe", "matmul", psO[k].ap[:, c, :], Psb[ib].ap[:, c, k * 128:(k + 1) * 128], Va[hb].ap[:, kj, 0:129],
                             start=(kj == 0 and c == 0), stop=(kj == NKV - 1), skip_group_check=True,
                             reads=[Psb[ib], Va[hb]], writes=[psO[k]])
            for k in range(ns):
                s = s0 + k
                S.op("dve", "reciprocal", rec.ap, psO[k].ap[:, :, 128], reads=[psO[k]], writes=[rec])
                S.op("dve", "tensor_tensor", nl.ap, rec.ap[:, 1:2], nlam.ap, ALU.mult, reads=[rec, nlam], writes=[nl])
                S.op("dve", "tensor_scalar", o1.ap, psO[k].ap[:, 0, 0:128], rec.ap[:, 0:1], None, ALU.mult, reads=[psO[k], rec], writes=[o1])
                S.op("dve", "scalar_tensor_tensor", out=o2.ap, in0=psO[k].ap[:, 1, 0:128], scalar=nl.ap, in1=o1.ap,
                     op0=ALU.mult, op1=ALU.add, reads=[psO[k], nl, o1], writes=[o2])
                S.op("dve", "scalar_tensor_tensor", out=ojunk.ap, in0=o2.ap, scalar=1.0, in1=o2.ap, op0=ALU.mult, op1=ALU.mult,
                     accum_out=oss.ap, reads=[o2], writes=[ojunk, oss])
                S.op("dve", "tensor_scalar", or1.ap, oss.ap, 1.0 / 128, EPS, ALU.mult, ALU.add, reads=[oss], writes=[or1])
                S.op("act", "activation", out=or2.ap, in_=or1.ap, func=AF.Sqrt, reads=[or1], writes=[or2])
                S.op("dve", "reciprocal", or1.ap, or2.ap, reads=[or2], writes=[or1])
                S.op("dve", "scalar_tensor_tensor", out=o_all[s].ap[:, hd * 128:(hd + 1) * 128], in0=o2.ap, scalar=or1.ap, in1=subg.ap,
                     op0=ALU.mult, op1=ALU.mult, reads=[o2, or1, subg], writes=[o_all[s]])
    S.barrier()
    A.reset(amark)

    wo = A.alloc("wo", [KC, D], BF16)
    stg = [A.alloc("wstg%d" % i, [D], F32) for i in range(2)]
    oT = [A.alloc("oT%d" % i, [KC, 128], BF16) for i in range(2)]
    load_cast_weight(P, wo_d, D, wo, stg)
    pT = P.pbank("pT", 0, BF16, [KC, 128])
    ps_o = [P.pbank("ps_o%d" % i, 1 + i) for i in range(2)]
    for s in range(NT):
        i = s % 2
        for kc in range(KC):
            S.op("pe", "transpose", pT.ap[:, kc, :], o_all[s].ap[:, kc * 128:(kc + 1) * 128], identb.ap,
                 reads=[o_all[s], identb], writes=[pT])
        S.op("act", "copy", oT[i].ap, pT.ap, reads=[pT], writes=[oT[i]])
        for n in range(2):
            for kc in range(KC):
                S.op("pe", "matmul", ps_o[n].ap, oT[i].ap[:, kc, :], wo.ap[:, kc, n * 512:(n + 1) * 512],
                     start=(kc == 0), stop=(kc == KC - 1), reads=[oT[i], wo], writes=[ps_o[n]])
            S.op("dve", "tensor_tensor", h[s].ap[:, n * 512:(n + 1) * 512], ps_o[n].ap, h[s].ap[:, n * 512:(n + 1) * 512], ALU.add,
                 reads=[ps_o[n], h[s]], writes=[h[s]])


def conv_layer(P, li, h, identb, onesm, ST):
    S, A, NT = P.S, P.A, P.NT
    pre = "c%d_" % li
    g_d = P.din(pre + "ng", [128, D])
    w1_d = P.din(pre + "w1", [D, 2 * D])
    b1_d = P.din(pre + "b1", [128, 16])
    dww_d = P.din(pre + "dww", [128, 8 * 31])
    dwb_d = P.din(pre + "dwb", [128, 8])
    lng_d = P.din(pre + "lng", [128, 8])
    lnb_d = P.din(pre + "lnb", [128, 8])
    w2_d = P.din(pre + "w2", [D, D])
    b2_d = P.din(pre + "b2", [128, D])

    wbig = A.alloc("wbig", [KC, 2 * D], BF16)
    w2 = A.alloc("w2", [KC, D], BF16)
    stg = [A.alloc("wstg%d" % i, [2 * D], F32) for i in range(2)]
    gB = A.alloc("gB", [D], F32)
    b2B = A.alloc("b2B", [D], F32)
    b1 = A.alloc("b1", [16], F32)
    dww = A.alloc("dww", [8, 31], F32)
    dwb = A.alloc("dwb", [8], F32)
    lng = A.alloc("lng", [8], F32)
    lnb = A.alloc("lnb", [8], F32)
    xn = [A.alloc("xn%d" % i, [D], BF16) for i in range(2)]
    junk = A.alloc("junk", [D], BF16)
    small = [[A.alloc("sm%d_%d" % (i, j), [1], F32) for j in range(3)] for i in range(2)]
    TT = ST * 128
    xnT = A.alloc("xnT", [KC, TT], BF16)
    u = [A.alloc("u%d" % i, [30 + TT], F32) for i in range(2)]
    uh = A.alloc("uh", [8, 30], F32)
    z = [A.alloc("z%d" % cc, [TT], F32) for cc in range(8)]
    t1 = [A.alloc("t1_%d" % i, [TT], F32) for i in range(2)]
    sg = [A.alloc("sg%d" % i, [TT], F32) for i in range(2)]
    mean = A.alloc("mean", [TT], F32)
    ez2 = A.alloc("ez2", [TT], F32)
    rstd = A.alloc("rstd", [TT], F32)
    yT = A.alloc("yT", [KC, TT], BF16)

    for (b, d) in ((gB, g_d), (b2B, b2_d), (b1, b1_d), (dwb, dwb_d), (lng, lng_d), (lnb, lnb_d)):
        S.dma(b, b.ap, d)
    S.dma(dww, dww.ap, dww_d.rearrange("p (c k) -> p c k", k=31))
    load_cast_weight(P, w1_d, 2 * D, wbig, stg)
    load_cast_weight(P, w2_d, D, w2, stg)
    S.op("pool", "memset", uh.ap, 0.0, writes=[uh])

    pT = P.pbank("pT", 0, BF16, [KC, 128])
    ps_g = P.pbank("ps_g", 1)
    ps_a = P.pbank("ps_a", 2)
    ps_m = P.pbank("ps_m", 3)
    ps_e = P.pbank("ps_e", 4)
    ps_o = [P.pbank("ps_o%d" % k, 5 + k) for k in range(2)]

    n_st = (NT + ST - 1) // ST
    it = 0
    for st in range(n_st):
        s0 = st * ST
        ns = min(ST, NT - s0)
        T = ns * 128
        for k in range(ns):
            s = s0 + k
            rmsnorm_T(P, h[s], gB, xn[it % 2], junk, small[it % 2], pT, identb,
                      (xnT, xnT.ap[:, :, k * 128:(k + 1) * 128]))
            it += 1
        for cc in range(8):
            ub = u[cc % 2]
            sgb = sg[cc % 2]
            S.op("pool", "tensor_copy", ub.ap[:, 0:30], uh.ap[:, cc, :],
                 reads=[uh], writes=[ub])
            for kc in range(KC):
                S.op("pe", "matmul", ps_g.ap[:, 0:T], wbig.ap[:, kc, D + cc * 128:D + (cc + 1) * 128],
                                                             xnT.ap[:, kc, 0:T], start=(kc == 0), stop=(kc == KC - 1),
                     reads=[wbig, xnT], writes=[ps_g])
            S.op("act", "activation", out=sgb.ap[:, 0:T], in_=ps_g.ap[:, 0:T], func=AF.Sigmoid,
                                                               bias=b1.ap[:, 8 + cc:9 + cc],
                 reads=[ps_g, b1], writes=[sgb])
            for kc in range(KC):
                S.op("pe", "matmul", ps_a.ap[:, 0:T], wbig.ap[:, kc, cc * 128:(cc + 1) * 128],
                                                             xnT.ap[:, kc, 0:T], start=(kc == 0), stop=(kc == KC - 1),
                     reads=[wbig, xnT], writes=[ps_a])
            S.op("dve", "scalar_tensor_tensor",
                out=ub.ap[:, 30:30 + T], in0=ps_a.ap[:, 0:T], scalar=b1.ap[:, cc:cc + 1], in1=sgb.ap[:, 0:T],
                op0=ALU.add, op1=ALU.mult, reads=[ps_a, b1, sgb], writes=[ub])
            S.op("pool", "tensor_copy", uh.ap[:, cc, :], ub.ap[:, T:T + 30],
                 reads=[ub], writes=[uh])
            ce = "dve"
            zb = z[cc]
            S.op(ce, "tensor_scalar", zb.ap[:, 0:T], ub.ap[:, 0:T], dww.ap[:, cc, 0:1],
                                                                     dwb.ap[:, cc:cc + 1], ALU.mult, ALU.add,
                 reads=[ub, dww, dwb], writes=[zb])
            for k in range(1, 31):
                S.op(ce, "scalar_tensor_tensor",
                    out=zb.ap[:, 0:T], in0=ub.ap[:, k:k + T], scalar=dww.ap[:, cc, k:k + 1], in1=zb.ap[:, 0:T],
                    op0=ALU.mult, op1=ALU.add, reads=[ub, dww, zb], writes=[zb])
            tb = t1[cc % 2]
            S.op("act", "activation", out=tb.ap[:, 0:T], in_=zb.ap[:, 0:T], func=AF.Square,
                 reads=[zb], writes=[tb])
            S.op("pe", "matmul", ps_m.ap[:, 0:T], onesm.ap, zb.ap[:, 0:T], start=(cc == 0), stop=(cc == 7),
                 reads=[onesm, zb], writes=[ps_m])
            S.op("pe", "matmul", ps_e.ap[:, 0:T], onesm.ap, tb.ap[:, 0:T], start=(cc == 0), stop=(cc == 7),
                 reads=[onesm, tb], writes=[ps_e])
        S.op("act", "copy", mean.ap[:, 0:T], ps_m.ap[:, 0:T], reads=[ps_m], writes=[mean])
        S.op("dve", "tensor_tensor", ez2.ap[:, 0:T], mean.ap[:, 0:T], mean.ap[:, 0:T], ALU.mult,
             reads=[mean], writes=[ez2])
        S.op("dve", "tensor_tensor", ez2.ap[:, 0:T], ps_e.ap[:, 0:T], ez2.ap[:, 0:T], ALU.subtract,
             reads=[ps_e, ez2], writes=[ez2])
        S.op("dve", "tensor_scalar", ez2.ap[:, 0:T], ez2.ap[:, 0:T], 0.0, EPS, ALU.max, ALU.add,
             reads=[ez2], writes=[ez2])
        S.op("act", "activation", out=rstd.ap[:, 0:T], in_=ez2.ap[:, 0:T], func=AF.Sqrt,
             reads=[ez2], writes=[rstd])
        S.op("dve", "reciprocal", rstd.ap[:, 0:T], rstd.ap[:, 0:T], reads=[rstd], writes=[rstd])
        for cc in range(8):
            tb = t1[cc % 2]
            zb = z[cc]
            S.op("dve", "tensor_tensor", tb.ap[:, 0:T], zb.ap[:, 0:T], mean.ap[:, 0:T], ALU.subtract,
                 reads=[zb, mean], writes=[tb])
            S.op("dve", "tensor_tensor", tb.ap[:, 0:T], tb.ap[:, 0:T], rstd.ap[:, 0:T], ALU.mult,
                 reads=[tb, rstd], writes=[tb])
            S.op("act", "activation", out=yT.ap[:, cc, 0:T], in_=tb.ap[:, 0:T], func=AF.Silu,
                                                             bias=lnb.ap[:, cc:cc + 1], scale=lng.ap[:, cc:cc + 1],
                 reads=[tb, lng, lnb], writes=[yT])
        if st == 0:
            P.debug("xnT", xnT, [128, KC, TT], BF16)
            P.debug("yT", yT, [128, KC, TT], BF16)
            P.debug("z0", z[0], [128, TT])
            P.debug("u1", u[1], [128, 30 + TT])
            P.debug("mean", mean, [128, TT])
            P.debug("rstd", rstd, [128, TT])
        for k in range(ns):
            s = s0 + k
            for n in range(2):
                for cc in range(8):
                    S.op("pe", "matmul", ps_o[n].ap, yT.ap[:, cc, k * 128:(k + 1) * 128],
                                                                    w2.ap[:, cc, n * 512:(n + 1) * 512],
                                                                    start=(cc == 0), stop=(cc == 7),
                         reads=[yT, w2], writes=[ps_o[n]])
                S.op("dve", "tensor_tensor", h[s].ap[:, n * 512:(n + 1) * 512], ps_o[n].ap,
                                                                h[s].ap[:, n * 512:(n + 1) * 512], ALU.add,
                     reads=[ps_o[n], h[s]], writes=[h[s]])
            S.op("pool", "tensor_tensor", h[s].ap, h[s].ap, b2B.ap, ALU.add,
                 reads=[h[s], b2B], writes=[h[s]])


def peer_layer(P, li, h, identb, identf, NG, ST):
    S, A, NT = P.S, P.A, P.NT
    pre = "f%d_" % li
    g_d = P.din(pre + "ng", [128, D])
    wq_d = P.din(pre + "wq", [D, 2 * D])
    sk_d = P.din(pre + "skT", [128, 256])
    U_d = P.din(pre + "u", [NEXP, D])
    V_d = P.din(pre + "v", [NEXP, D])
    st_s2 = P.dscratch(pre + "st_s2", [NT, 128, 8 * 128 + 8])
    st_s1 = P.dscratch(pre + "st_s1", [NT, 32, 128, 32])
    st_s2_b = Buf("st_s2", st_s2)
    st_s1_b = Buf("st_s1", st_s1)

    xnT_all = [A.alloc("xnTa%d" % s, [KC, 128], BF16) for s in range(NT)]
    peer_mark = A.mark()

    wbig = A.alloc("wq", [KC, 2 * D], BF16)
    stg = [A.alloc("wstg%d" % i, [2 * D], F32) for i in range(1)]
    gB = A.alloc("gB", [D], F32)
    skf = A.alloc("skf", [256], F32)
    skb = A.alloc("skb", [2, 128], BF16)
    xn = [A.alloc("xn%d" % i, [D], BF16) for i in range(2)]
    junk = A.alloc("junk", [D], BF16)
    small = [[A.alloc("sm%d_%d" % (i, j), [1], F32) for j in range(3)] for i in range(2)]
    TT = ST * 128
    qT = A.alloc("qT", [16, TT], BF16)
    s_sb = A.alloc("s_sb", [16, 128], F32)
    s_tmp = A.alloc("s_tmp", [128], F32)
    vtop = A.alloc("vtop", [16, 16], F32)
    v1b = A.alloc("v1b", [8, 16], F32)
    cs = A.alloc("cs", [8, 256], F32)
    cs_tmp = A.alloc("cs_tmp", [256], F32)
    top_s = A.alloc("top_s", [8, 16], F32)
    ee = A.alloc("ee", [8, 16], F32)
    mx = A.alloc("mx", [1], F32)
    nmx = A.alloc("nmx", [1], F32)
    zz = A.alloc("zz", [8], F32)
    lnz = A.alloc("lnz", [8], F32)
    bh = A.alloc("bh", [8], F32)
    rec2 = [A.alloc("rec2_%d" % i, [8 * 128 + 8], F32) for i in range(1)]
    rec1 = [A.alloc("rec1_%d" % i, [32, 8, 4], F32) for i in range(1)]

    S.dma(gB, gB.ap, g_d)
    S.dma(skf, skf.ap, sk_d)
    S.op("dve", "tensor_copy", skb.ap, skf.ap.rearrange("p (a b) -> p a b", b=128), reads=[skf], writes=[skb])
    load_cast_weight(P, wq_d, 2 * D, wbig, stg)

    pT = P.pbank("pT", 0, BF16, [KC, 128])
    ps_q = [P.pbank("ps_q%d" % i, 1 + i) for i in range(2)]
    ps_s = [P.pbank("ps_s%d" % i, 3 + i, F32, [4, 128]) for i in range(4)]

    n_st = (NT + ST - 1) // ST
    it = 0
    for st in range(n_st):
        s0 = st * ST
        ns = min(ST, NT - s0)
        T = ns * 128
        for k in range(ns):
            s = s0 + k
            rmsnorm_T(P, h[s], gB, xn[it % 2], junk, small[it % 2], pT, identb, (xnT_all[s], xnT_all[s].ap))
            it += 1
        for j in range(16):
            pq = ps_q[j % 2]
            for k in range(ns):
                for kc in range(KC):
                    S.op("pe", "matmul",
                        pq.ap[:, k * 128:(k + 1) * 128], wbig.ap[:, kc, j * 128:(j + 1) * 128],
                        xnT_all[s0 + k].ap[:, kc, :], start=(kc == 0), stop=(kc == KC - 1),
                        reads=[wbig, xnT_all[s0 + k]], writes=[pq])
            if j % 2 == 0:
                S.op("act", "copy", qT.ap[:, j, 0:T], pq.ap[:, 0:T], reads=[pq], writes=[qT])
            else:
                S.op("dve", "tensor_copy", qT.ap[:, j, 0:T], pq.ap[:, 0:T], reads=[pq], writes=[qT])
        for k in range(ns):
            s = s0 + k
            r2 = rec2[0]
            r1 = rec1[0]
            for q4 in range(4):
                for jj in range(4):
                    j = q4 * 4 + jj
                    S.op("pe", "matmul",
                        ps_s[q4].ap[:, jj, :], qT.ap[:, j, k * 128:(k + 1) * 128], skb.ap[:, j % 2, :],
                        start=True, stop=True, reads=[qT, skb], writes=[ps_s[q4]])
                if q4 % 2 == 0:
                    S.op("act", "copy", s_sb.ap[:, q4 * 4:(q4 + 1) * 4, :], ps_s[q4].ap,
                         reads=[ps_s[q4]], writes=[s_sb])
                else:
                    S.op("dve", "tensor_copy", s_sb.ap[:, q4 * 4:(q4 + 1) * 4, :], ps_s[q4].ap,
                         reads=[ps_s[q4]], writes=[s_sb])
            for j in range(16):
                S.op("dve", "max", out=vtop.ap[:, j, 0:8], in_=s_sb.ap[:, j, :], reads=[s_sb], writes=[vtop])
                S.op("dve", "match_replace", out=s_tmp.ap, in_to_replace=vtop.ap[:, j, 0:8],
                                                           in_values=s_sb.ap[:, j, :], imm_value=NEG,
                     reads=[s_sb, vtop], writes=[s_tmp])
                S.op("dve", "max", out=vtop.ap[:, j, 8:16], in_=s_tmp.ap, reads=[s_tmp], writes=[vtop])
            vt4 = vtop.ap.rearrange("p (h c) k -> p h c k", c=2)
            v1 = vt4[:, :, 0, :]
            v2 = vt4[:, :, 1, :]
            cs4 = cs.ap.rearrange("p h (a b) -> p h a b", b=16)

            def top16_cs(dst_first=None):
                for hh in range(8):
                    S.op("dve", "max", out=top_s.ap[:, hh, 0:8], in_=cs.ap[:, hh, :], reads=[cs], writes=[top_s])
                    S.op("dve", "match_replace", out=cs_tmp.ap, in_to_replace=top_s.ap[:, hh, 0:8],
                                                                 in_values=cs.ap[:, hh, :], imm_value=NEG,
                         reads=[cs, top_s], writes=[cs_tmp])
                    S.op("dve", "max", out=top_s.ap[:, hh, 8:16], in_=cs_tmp.ap, reads=[cs_tmp], writes=[top_s])

            S.op("dve", "tensor_tensor", cs4, bcast(v1, 3, [128, 8, 16, 16]), bcast(v2, 2, [128, 8, 16, 16]), ALU.add,
                 reads=[vtop], writes=[cs])
            top16_cs()
            S.op("dve", "tensor_reduce", out=mx.ap, in_=top_s.ap[:, :, 0], axis=AX.X, op=ALU.max, reads=[top_s], writes=[mx])
            S.op("dve", "tensor_scalar", nmx.ap, mx.ap, -1.0, None, ALU.mult, reads=[mx], writes=[nmx])
            S.op("act", "activation", out=ee.ap, in_=top_s.ap, func=AF.Exp, bias=nmx.ap, reads=[top_s, nmx], writes=[ee])
            S.op("dve", "tensor_reduce", out=zz.ap, in_=ee.ap, axis=AX.X, op=ALU.add, reads=[ee], writes=[zz])
            S.op("act", "activation", out=lnz.ap, in_=zz.ap, func=AF.Ln, reads=[zz], writes=[lnz])
            S.op("dve", "tensor_scalar", bh.ap, lnz.ap, -1.0, nmx.ap, ALU.mult, ALU.add, reads=[lnz, nmx], writes=[bh])
            S.op("dve", "tensor_tensor", v1b.ap, v1, bcast(bh.ap, 2, [128, 8, 16]), ALU.add, reads=[vtop, bh], writes=[v1b])
            S.op("dve", "tensor_tensor", cs4, bcast(v1b.ap, 3, [128, 8, 16, 16]), bcast(v2, 2, [128, 8, 16, 16]), ALU.add,
                 reads=[v1b, vtop, top_s], writes=[cs])
            top16_cs()
            s4 = s_sb.ap.rearrange("p (h c) k -> p h c k", c=2)
            S.op("pool", "tensor_copy", r2.ap[:, 0:1024].rearrange("p (h k) -> p h k", k=128), s4[:, :, 1, :],
                 reads=[s_sb], writes=[r2])
            S.op("dve", "tensor_copy", r2.ap[:, 1024:1032], top_s.ap[:, :, 15], reads=[top_s], writes=[r2])
            S.op("dve", "tensor_tensor", r1.ap.transpose([0, 2, 1, 3]),
                                                        s4[:, :, 0, :].rearrange("p h (g i) -> p h g i", i=4),
                                                        bcast(bcast(bh.ap, 2, [128, 8, 32]), 3, [128, 8, 32, 4]), ALU.add,
                 reads=[s_sb, bh], writes=[r1])
            S.dma(st_s2_b, st_s2[s], r2.ap, reads=[r2])
            S.dma(st_s1_b, st_s1[s].rearrange("g p x -> p g x"), r1.ap.rearrange("p g h i -> p g (h i)"), reads=[r1])

    S.barrier()
    A.reset(peer_mark)

    UT = [A.alloc("UT%d" % i, [KC, 512], BF16) for i in range(2)]
    Vb = [A.alloc("Vb%d" % i, [4, D], BF16) for i in range(2)]
    stU = [A.alloc("stU%d" % i, [D], F32) for i in range(2)]
    stV = [A.alloc("stV%d" % i, [D], F32) for i in range(2)]
    rs2 = [A.alloc("rs2_%d" % i, [8 * 128 + 8], F32) for i in range(2)]
    rs1 = [A.alloc("rs1_%d" % i, [8, 4], F32) for i in range(2)]
    cand = [A.alloc("cand%d" % i, [4, 512], F32) for i in range(2)]
    ex = [A.alloc("ex%d" % i, [4, 512], BF16) for i in range(2)]
    mk = [A.alloc("mk%d" % i, [4, 512], BF16) for i in range(2)]
    gA = [A.alloc("gA%d" % i, [512], BF16) for i in range(2)]
    Gs = A.alloc("Gs", [512], F32)
    GA = [A.alloc("GA%d" % i, [512], BF16) for i in range(2)]
    GAT = [A.alloc("GAT%d" % i, [4, 128], BF16) for i in range(2)]

    ps_A = [P.pbank("ps_A%d" % i, [0, 2][i]) for i in range(2)]
    ps_T = [P.pbank("ps_T%d" % i, [1, 7][i], F32, [4, 128]) for i in range(2)]
    ps_O = [P.pbank("ps_O%d" % i, 3 + i) for i in range(2)]
    _psu = P.pbank("ps_U", 5, F32, [4, 128])
    ps_U = [_psu, _psu]
    ps_G = P.pbank("ps_G", 6, BF16, [4, 128])

    Uv = U_d.rearrange("(g j p) c -> g j p c", j=4, p=128)
    Vv = V_d.rearrange("(g j p) c -> g j p c", j=4, p=128)

    def table_prep(g):
        UTg = UT[g % 2]
        Vg = Vb[g % 2]
        for j in range(4):
            su = stU[j % 2]
            sv = stV[j % 2]
            S.dma(su, su.ap, Uv[g, j])
            S.dma(sv, sv.ap, Vv[g, j])
            S.op("pool", "tensor_copy", Vg.ap[:, j, :], sv.ap, reads=[sv], writes=[Vg])
            for half in range(2):
                pu = ps_U[half]
                for q in range(4):
                    kc = half * 4 + q
                    S.op("pe", "transpose", pu.ap[:, q, :], su.ap[:, kc * 128:(kc + 1) * 128], identf.ap,
                         reads=[su, identf], writes=[pu])
                S.op("act", "copy", UTg.ap[:, half * 4:(half + 1) * 4, j * 128:(j + 1) * 128], pu.ap,
                     reads=[pu], writes=[UTg])

    def a_matmuls(g, s, ib):
        UTg = UT[g % 2]
        pa = ps_A[ib]
        for kc in range(KC):
            S.op("pe", "matmul", pa.ap, xnT_all[s].ap[:, kc, :], UTg.ap[:, kc, :],
                 start=(kc == 0), stop=(kc == KC - 1), reads=[UTg, xnT_all[s]], writes=[pa])

    iters = [(g, s) for g in range(NG) for s in range(NT)]
    ps_O2 = [ps_O, [P.pbank("ps_Ob%d" % i, [1, 7][i]) for i in range(2)]]

    def finish(n):
        g, s = iters[n]
        ib = n % 2
        Vg = Vb[g % 2]
        po = ps_O2[ib]
        S.op("act", "copy", GAT[ib].ap, ps_G.ap, reads=[ps_G], writes=[GAT[ib]])
        for nn in range(2):
            for j in range(4):
                S.op("pe", "matmul", po[nn].ap, GAT[ib].ap[:, j, :], Vg.ap[:, j, nn * 512:(nn + 1) * 512],
                     start=(j == 0), stop=(j == 3), reads=[GAT[ib], Vg], writes=[po[nn]])

    def hadd(n):
        g, s = iters[n]
        po = ps_O2[n % 2]
        for nn in range(2):
            S.op("dve", "tensor_tensor", h[s].ap[:, nn * 512:(nn + 1) * 512], po[nn].ap,
                 h[s].ap[:, nn * 512:(nn + 1) * 512], ALU.add, reads=[po[nn], h[s]], writes=[h[s]])

    table_prep(0)
    a_matmuls(0, 0, 0)
    for n, (g, s) in enumerate(iters):
        ib = n % 2
        r2 = rs2[ib]
        r1 = rs1[ib]
        S.dma(r2, r2.ap, st_s2[s], reads=[st_s2_b])
        S.dma(r1, r1.ap.rearrange("p h i -> p (h i)"), st_s1[s, g], reads=[st_s1_b])
        s2v = r2.ap[:, 0:1024].rearrange("p (h k) -> p h k", k=128)
        thr = r2.ap[:, 1024:1032]
        for hh in range(2):
            c4 = cand[hh].ap.rearrange("p h (i k) -> p h i k", k=128)
            S.op("pool", "tensor_tensor", c4, bcast(r1.ap[:, hh * 4:(hh + 1) * 4, :], 3, [128, 4, 4, 128]),
                 bcast(s2v[:, hh * 4:(hh + 1) * 4, :], 2, [128, 4, 4, 128]), ALU.add, reads=[r1, r2], writes=[cand[hh]])
            S.op("act", "activation", out=ex[hh].ap, in_=cand[hh].ap, func=AF.Exp, reads=[cand[hh]], writes=[ex[hh]])
            for h4 in range(4):
                hd = hh * 4 + h4
                S.op("dve", "scalar_tensor_tensor", out=mk[hh].ap[:, h4, :], in0=cand[hh].ap[:, h4, :], scalar=thr[:, hd:hd + 1],
                     in1=ex[hh].ap[:, h4, :], op0=ALU.is_ge, op1=ALU.mult, reads=[cand[hh], ex[hh], r2], writes=[mk[hh]])
        if s == NT // 2 and g + 1 < NG:
            table_prep(g + 1)
        if n + 1 < len(iters):
            a_matmuls(iters[n + 1][0], iters[n + 1][1], (n + 1) % 2)
        S.op("act", "activation", out=gA[ib].ap, in_=ps_A[ib].ap, func=AF.Gelu, reads=[ps_A[ib]], writes=[gA[ib]])
        if n >= 1:
            finish(n - 1)
        S.op("dve", "tensor_tensor", mk[0].ap, mk[0].ap, mk[1].ap, ALU.add, reads=[mk[0], mk[1]], writes=[mk[0]])
        S.op("dve", "tensor_tensor", mk[0].ap[:, 0:2, :], mk[0].ap[:, 0:2, :], mk[0].ap[:, 2:4, :], ALU.add, reads=[mk[0]], writes=[mk[0]])
        S.op("dve", "tensor_tensor", Gs.ap, mk[0].ap[:, 0, :], mk[0].ap[:, 1, :], ALU.add, reads=[mk[0]], writes=[Gs])
        S.op("dve", "tensor_tensor", GA[ib].ap, Gs.ap, gA[ib].ap, ALU.mult, reads=[Gs, gA[ib]], writes=[GA[ib]])
        if n >= 1:
            hadd(n - 1)
        for j in range(4):
            S.op("pe", "transpose", ps_G.ap[:, j, :], GA[ib].ap[:, j * 128:(j + 1) * 128], identb.ap,
                 reads=[GA[ib], identb], writes=[ps_G])
    finish(len(iters) - 1)
    hadd(len(iters) - 1)


def _rep(v):
    return np.ascontiguousarray(np.broadcast_to(np.asarray(v, np.float32)[None, :], (128, v.shape[0])))


def _fm(v, nch):
    return np.ascontiguousarray(np.asarray(v, np.float32).reshape(nch, 128).T)


def conv_inputs(li, i, inp):
    pre = "c%d_" % li
    return {
        pre + "ng": _rep(inp["a_norm_g"][i]),
        pre + "w1": np.ascontiguousarray(inp["a_pw1_w"][i]),
        pre + "b1": _fm(inp["a_pw1_b"][i], 16),
        pre + "dww": np.ascontiguousarray(inp["a_dw_w"][i].reshape(31, 8, 128).transpose(2, 1, 0).reshape(128, 8 * 31)),
        pre + "dwb": _fm(inp["a_dw_b"][i], 8),
        pre + "lng": _fm(inp["a_ln_g"][i], 8),
        pre + "lnb": _fm(inp["a_ln_b"][i], 8),
        pre + "w2": np.ascontiguousarray(inp["a_pw2_w"][i]),
        pre + "b2": _rep(inp["a_pw2_b"][i]),
    }


def peer_inputs(li, layer, inp):
    pre = "f%d_" % li
    skT = np.concatenate([inp["f_subkey1"][layer].T, inp["f_subkey2"][layer].T], axis=1)
    return {
        pre + "ng": _rep(inp["f_norm_g"][layer]),
        pre + "wq": np.ascontiguousarray(inp["f_wq"][layer]),
        pre + "skT": np.ascontiguousarray(skT.astype(np.float32)),
        pre + "u": np.ascontiguousarray(inp["f_u"][layer]),
        pre + "v": np.ascontiguousarray(inp["f_v"][layer]),
    }


ROPE_THETA = 500000.0


def rope_table(pos):
    inv = (np.float32(ROPE_THETA) ** (-np.arange(0, 16, 2, dtype=np.float32) / np.float32(16))).astype(np.float32)
    ang = pos.astype(np.float32)[:, None] * inv[None, :]
    return np.concatenate([np.cos(ang), np.sin(ang)], axis=1).astype(np.float32)


def _pl(tab, nt):
    c = tab.shape[1]
    return np.ascontiguousarray(tab.reshape(nt, 128, c).transpose(1, 0, 2).reshape(128, nt * c))


def kv_inputs(inp, hfull, nkv):
    return {
        "hfull": np.ascontiguousarray(hfull),
        "kv_ng": _rep(inp["kv_norm_g"]),
        "kv_w": np.ascontiguousarray(inp["w_kv"]),
        "kv_cs": _pl(rope_table(np.arange(nkv * 128)), nkv),
    }


def attn_inputs(li, j, inp, pos0, nt, nkv):
    pre = "b%d_" % li
    pos = np.arange(pos0, pos0 + nt * 128)
    lam = np.concatenate([inp["b_lambda_q1"][j], inp["b_lambda_k1"][j], inp["b_lambda_q2"][j], inp["b_lambda_k2"][j]])
    ekj = np.zeros((nkv, nkv * 128), np.float32)
    for kj in range(nkv):
        ekj[kj, kj * 128:(kj + 1) * 128] = 1.0
    qtile = pos // 128
    bpen = (-30000.0 * (qtile[None, :] < np.arange(nkv)[:, None])).astype(np.float32)
    tri_m = (-30000.0 * (np.arange(128)[:, None] > np.arange(128)[None, :])).astype(np.float32)
    zero = np.zeros((128, 128), np.float32)
    if pos0 == 0:
        tri = np.concatenate([tri_m, zero], axis=1)
    else:
        tri = np.concatenate([zero, tri_m], axis=1)
    return {
        pre + "ng": _rep(inp["b_norm_g"][j]),
        pre + "wq": np.ascontiguousarray(inp["b_wq"][j]),
        pre + "wo": np.ascontiguousarray(inp["b_wo"][j]),
        pre + "lam": _rep(lam),
        pre + "subg": _rep(inp["b_subln_g"][j]),
        pre + "qcs": _pl(rope_table(pos), nt),
        pre + "ekj": ekj,
        pre + "bpen": np.ascontiguousarray(bpen),
        pre + "tri": np.ascontiguousarray(tri),
    }


def lam_init_of(layer):
    return 0.8 - 0.6 * math.exp(-0.3 * layer)


NT_CORE = 17


def kernel(**inp):
    inp = {k: np.asarray(v) for k, v in inp.items()}
    x = inp["x"].astype(np.float32)
    h0 = np.zeros((BATCH, LP, D), np.float32)
    h0[:, :N_META] = inp["meta_tokens"][None]
    h0[:, N_META:N_META + SEQ] = x
    eye = np.eye(128, dtype=np.float32)
    NT = NT_CORE
    t0s = [0, NTB - NT]

    layers = [{"kind": "conv"}, {"kind": "conv"},
              {"kind": "attn", "lam_init": lam_init_of(2)}, {"kind": "attn", "lam_init": lam_init_of(3)}]
    P = build_prog(NT, layers, kv=True, NKV=NTB, kv_after=2, with_final=True)
    shared = {"ident": eye, "fin_g": _rep(inp["final_norm_g"]),
              "kv_ng": _rep(inp["kv_norm_g"]), "kv_w": np.ascontiguousarray(inp["w_kv"]),
              "kv_cs": _pl(rope_table(np.arange(NTB * 128)), NTB)}
    for li in range(2):
        shared.update(conv_inputs(li, li, inp))
    for li in range(4):
        shared.update(peer_inputs(li, li, inp))
    in_maps = []
    for core in range(8):
        b, half = core // 2, core % 2
        m = dict(shared)
        m["hin"] = np.ascontiguousarray(h0[b, t0s[half] * 128:(t0s[half] + NT) * 128])
        for li in (2, 3):
            m.update(attn_inputs(li, li - 2, inp, t0s[half] * 128, NT, NTB))
        idx = np.zeros((128, NTB), np.int32)
        for t in range(NTB):
            if t < NT:
                row0 = ((2 * b) * NT + t) * 128
            else:
                row0 = ((2 * b + 1) * NT + (t - (NTB - NT))) * 128
            idx[:, t] = row0 + np.arange(128)
        m["kv_idx"] = idx
        in_maps.append(m)
    res = run_bass_kernel_spmd(P.nc, in_maps, core_ids=list(range(8)))
    outs = [r["hout"] for r in res.results]
    full = np.zeros((BATCH, LP, D), np.float32)
    for core in range(8):
        b, half = core // 2, core % 2
        if half == 0:
            full[b, 0:NT * 128] = outs[core]
        else:
            full[b, NT * 128:] = outs[core][128:]
    return np.ascontiguousarray(full[:, N_META:N_META + SEQ])
```
